# Optimizing a Trainium2 kernel written in Bass

```python
import jax, jax.numpy as jnp
from jax import lax
import numpy as np

D_MODEL = 2048
BATCH = 8
SEQ = 2048
DEPTH = 1

PLE_DIM = 256
D_FF = 5632
FOX_HEADS = 8
FOX_HEAD_DIM = 128
FOX_WIDTH = FOX_HEADS * FOX_HEAD_DIM
HGRN_HEADS = 8
HGRN_KEY_DIM = 128
HGRN_VAL_DIM = 128
HGRN_KEY_WIDTH = HGRN_HEADS * HGRN_KEY_DIM
HGRN_VAL_WIDTH = HGRN_HEADS * HGRN_VAL_DIM
Q_BLOCK = 128
CHUNK = 64
NORM_EPS = 1e-6
MACARON_SCALE = 0.5
SPLIT_SIZES = (FOX_WIDTH, FOX_WIDTH, FOX_WIDTH, FOX_HEADS,
               HGRN_KEY_WIDTH, HGRN_KEY_WIDTH, HGRN_VAL_WIDTH, HGRN_VAL_WIDTH,
               D_MODEL, D_MODEL)
IN_WIDTH = int(sum(SPLIT_SIZES))
SPLIT_POINTS = tuple(int(v) for v in np.cumsum(SPLIT_SIZES)[:-1])

kernel_name = 'hybrid_fox_hgrn2_macaron_sandwich_ple'


def rms_norm(x, g):
    xf = x.astype(jnp.float32)
    y = xf * lax.rsqrt(jnp.mean(xf * xf, axis=-1, keepdims=True) + NORM_EPS)
    return (y * g.astype(jnp.float32)).astype(x.dtype)


def swiglu(u, w_gate, w_up, w_down):
    return (jax.nn.silu(u @ w_gate) * (u @ w_up)) @ w_down


def fox_attention(q, k, v, log_f):
    b, s, h, dh = q.shape
    nb = s // Q_BLOCK
    c = jnp.cumsum(log_f, axis=1).transpose(0, 2, 1)
    qb = q.reshape(b, nb, Q_BLOCK, h, dh).transpose(1, 0, 2, 3, 4)
    cb = c.reshape(b, h, nb, Q_BLOCK).transpose(2, 0, 1, 3)
    kpos = jnp.arange(s)
    scale = FOX_HEAD_DIM ** -0.5

    def block(args):
        qi, ci, bi = args
        logits = jnp.einsum('bqhd,bkhd->bhqk', qi, k,
                            preferred_element_type=jnp.float32) * scale
        logits = logits + ci[:, :, :, None] - c[:, :, None, :]
        qpos = bi * Q_BLOCK + jnp.arange(Q_BLOCK)
        causal = kpos[None, :] <= qpos[:, None]
        logits = jnp.where(causal, logits, -jnp.inf)
        probs = jax.nn.softmax(logits, axis=-1)
        return jnp.einsum('bhqk,bkhd->bqhd', probs.astype(v.dtype), v)

    out = lax.map(block, (qb, cb, jnp.arange(nb)))
    return out.transpose(1, 0, 2, 3, 4).reshape(b, s, h, dh)


def hgrn2_recurrence(q, k, v, log_f):
    b, s, h, dk = q.shape
    dv = v.shape[-1]
    n = s // CHUNK

    def chunks(t):
        return t.astype(jnp.float32).reshape(b, n, CHUNK, h, t.shape[-1]).transpose(1, 0, 3, 2, 4)

    causal = jnp.tril(jnp.ones((CHUNK, CHUNK), dtype=bool))

    def step(state, inp):
        qc, kc, vc, lfc = inp
        cum = jnp.cumsum(lfc, axis=2)
        inter = jnp.einsum('bhtk,bhkv->bhtv', qc * jnp.exp(cum), state)
        rel = jnp.where(causal[:, :, None],
                        cum[:, :, :, None, :] - cum[:, :, None, :, :], -jnp.inf)
        scores = jnp.einsum('bhtk,bhsk,bhtsk->bhts', qc, kc, jnp.exp(rel))
        intra = jnp.einsum('bhts,bhsv->bhtv', scores, vc)
        last = cum[:, :, -1:, :]
        new_state = (state * jnp.exp(last[:, :, 0, :, None])
                     + jnp.einsum('bhsk,bhsv->bhkv', kc * jnp.exp(last - cum), vc))
        return new_state, inter + intra

    state0 = jnp.zeros((b, h, dk, dv), jnp.float32)
    _, out = lax.scan(step, state0, (chunks(q), chunks(k), chunks(v), chunks(log_f)))
    return out.transpose(1, 0, 3, 2, 4).reshape(b, s, h, dv)


def token_mixing(u, w_in, fox_f_bias, lower_bound, hgrn_norm_g, w_proj_fox, w_proj_hgrn, w_out):
    b, s, _ = u.shape
    proj = u @ w_in
    q_a, k_a, v_a, f_a, q_b, f_b, i_b, g_b, gate_a, gate_b = jnp.split(proj, SPLIT_POINTS, axis=-1)
    log_fa = jax.nn.log_sigmoid((f_a + fox_f_bias).astype(jnp.float32))
    y_a = fox_attention(q_a.reshape(b, s, FOX_HEADS, FOX_HEAD_DIM),
                        k_a.reshape(b, s, FOX_HEADS, FOX_HEAD_DIM),
                        v_a.reshape(b, s, FOX_HEADS, FOX_HEAD_DIM), log_fa)
    y_a = y_a.reshape(b, s, FOX_WIDTH) @ w_proj_fox
    f = lower_bound + (1.0 - lower_bound) * jax.nn.sigmoid(f_b.astype(jnp.float32))
    o_b = hgrn2_recurrence(jax.nn.silu(q_b).reshape(b, s, HGRN_HEADS, HGRN_KEY_DIM),
                           (1.0 - f).reshape(b, s, HGRN_HEADS, HGRN_KEY_DIM),
                           i_b.reshape(b, s, HGRN_HEADS, HGRN_VAL_DIM),
                           jnp.log(f).reshape(b, s, HGRN_HEADS, HGRN_KEY_DIM))
    o_b = rms_norm(o_b, hgrn_norm_g).astype(u.dtype) * jax.nn.silu(g_b.reshape(b, s, HGRN_HEADS, HGRN_VAL_DIM))
    y_b = o_b.reshape(b, s, HGRN_VAL_WIDTH) @ w_proj_hgrn
    merged = jax.nn.sigmoid(gate_a) * y_a + jax.nn.sigmoid(gate_b) * y_b
    return merged @ w_out


def setup_inputs(seed: int = 0) -> dict:
    key = jax.random.key(seed)
    ks = jax.random.split(key, 25)
    f32 = jnp.float32

    def w(k, shape, fan_in):
        return jax.random.normal(k, shape, f32) * (fan_in ** -0.5)

    def gain(k, shape):
        return 1.0 + 0.02 * jax.random.normal(k, shape, f32)

    L, D, F = DEPTH, D_MODEL, D_FF
    return {
        'x': jax.random.normal(ks[0], (BATCH, SEQ, D), f32),
        'p': jax.random.normal(ks[1], (L, BATCH, SEQ, PLE_DIM), f32),
        'ffn1_pre_g': gain(ks[2], (L, D)),
        'ffn1_post_g': gain(ks[3], (L, D)),
        'ffn1_w_gate': w(ks[4], (L, D, F), D),
        'ffn1_w_up': w(ks[5], (L, D, F), D),
        'ffn1_w_down': w(ks[6], (L, F, D), F),
        'mix_pre_g': gain(ks[7], (L, D)),
        'mix_post_g': gain(ks[8], (L, D)),
        'mix_w_in': w(ks[9], (L, D, IN_WIDTH), D),
        'fox_f_bias': 0.1 * jax.random.normal(ks[10], (L, FOX_HEADS), f32),
        'hgrn_lb_logits': 0.1 * jax.random.normal(ks[11], (L + 1, HGRN_KEY_WIDTH), f32),
        'hgrn_norm_g': gain(ks[12], (L, HGRN_VAL_DIM)),
        'mix_w_proj_fox': w(ks[13], (L, FOX_WIDTH, D), FOX_WIDTH),
        'mix_w_proj_hgrn': w(ks[14], (L, HGRN_VAL_WIDTH, D), HGRN_VAL_WIDTH),
        'mix_w_out': w(ks[15], (L, D, D), D),
        'ffn2_pre_g': gain(ks[16], (L, D)),
        'ffn2_post_g': gain(ks[17], (L, D)),
        'ffn2_w_gate': w(ks[18], (L, D, F), D),
        'ffn2_w_up': w(ks[19], (L, D, F), D),
        'ffn2_w_down': w(ks[20], (L, F, D), F),
        'ple_pre_g': gain(ks[21], (L, D)),
        'ple_post_g': gain(ks[22], (L, D)),
        'ple_w_gate': w(ks[23], (L, D, D), D),
        'ple_w_proj': w(ks[24], (L, PLE_DIM, D), PLE_DIM),
    }


def reference(x, p, ffn1_pre_g, ffn1_post_g, ffn1_w_gate, ffn1_w_up, ffn1_w_down,
              mix_pre_g, mix_post_g, mix_w_in, fox_f_bias, hgrn_lb_logits, hgrn_norm_g,
              mix_w_proj_fox, mix_w_proj_hgrn, mix_w_out,
              ffn2_pre_g, ffn2_post_g, ffn2_w_gate, ffn2_w_up, ffn2_w_down,
              ple_pre_g, ple_post_g, ple_w_gate, ple_w_proj):
    lower_bounds = jnp.cumsum(jax.nn.softmax(hgrn_lb_logits.astype(jnp.float32), axis=0), axis=0)
    h = x
    for i in range(DEPTH):
        h = h + MACARON_SCALE * rms_norm(
            swiglu(rms_norm(h, ffn1_pre_g[i]), ffn1_w_gate[i], ffn1_w_up[i], ffn1_w_down[i]),
            ffn1_post_g[i])
        h = h + rms_norm(
            token_mixing(rms_norm(h, mix_pre_g[i]), mix_w_in[i], fox_f_bias[i], lower_bounds[i],
                         hgrn_norm_g[i], mix_w_proj_fox[i], mix_w_proj_hgrn[i], mix_w_out[i]),
            mix_post_g[i])
        h = h + MACARON_SCALE * rms_norm(
            swiglu(rms_norm(h, ffn2_pre_g[i]), ffn2_w_gate[i], ffn2_w_up[i], ffn2_w_down[i]),
            ffn2_post_g[i])
        u = rms_norm(h, ple_pre_g[i])
        h = h + rms_norm(jax.nn.sigmoid(u @ ple_w_gate[i]) * (p[i] @ ple_w_proj[i]), ple_post_g[i])
    return h
```

```python
import numpy as np
from contextlib import ExitStack
import concourse.bass as bass
import concourse.mybir as mybir
from concourse.bass_utils import run_bass_kernel_spmd

F32 = mybir.dt.float32
BF16 = mybir.dt.bfloat16
AF = mybir.ActivationFunctionType
ALU = mybir.AluOpType

D = 2048
S = 2048
F = 5632
NH = 8
DH = 128
PLE = 256
INW = 11272
TB = 512
NB = S // TB
KC = D // 128
EPS = 1e-6
CELL = 1024
C_QA, C_KA, C_VA, C_FA, C_QB, C_FB, C_IB, C_GB, C_GA, C_GBT = 0, 1024, 2048, 3072, 3080, 4104, 5128, 6152, 7176, 9224
FGROUPS = [12, 10, 12, 10]
M_QA, M_KA, M_VA, M_QB, M_FB, M_IB, M_GB, M_GA, M_GBT = 0, 8, 16, 24, 32, 40, 48, 56, 72
NSLOT = 6


class O:
    __slots__ = ("ap", "cells")

    def __init__(self, ap, cells):
        self.ap = ap
        self.cells = cells


class Buf:
    def __init__(self, ap, fshape, esize, key, byte0=0, cell=CELL):
        self.ap = ap
        self.cell = cell
        self.fshape = tuple(fshape)
        self.es = esize
        self.key = key
        self.byte0 = byte0
        st = []
        acc = 1
        for n in reversed(self.fshape):
            st.append(acc)
            acc *= n
        self.strides = list(reversed(st))

    @property
    def cells(self):
        return self.s().cells

    def s(self, *idx, p=None):
        full = [slice(None)] * len(self.fshape)
        for i, x in enumerate(idx):
            full[i] = x
        lo = hi = 0
        for n, stp, x in zip(self.fshape, self.strides, full):
            if isinstance(x, int):
                lo += x * stp
                hi += x * stp
            else:
                a = x.start or 0
                b = n if x.stop is None else x.stop
                lo += a * stp
                hi += (b - 1) * stp
        c0 = (self.byte0 + lo * self.es) // self.cell
        c1 = (self.byte0 + hi * self.es + self.es - 1) // self.cell
        psl = slice(None) if p is None else slice(p[0], p[1])
        return O(self.ap[(psl,) + tuple(full)], [(self.key, c) for c in range(c0, c1 + 1)])


class Sched:
    ENG = ("pe", "act", "dve", "pool", "sp")

    def __init__(self, nc, es):
        self.nc = nc
        self.es = es
        self.q = {e: [] for e in self.ENG}
        self.sems = {}
        self.cnt = {}
        self.waited = {e: {} for e in self.ENG}
        self.lastw = {}
        self.readers = {}
        for e in ("pe", "act", "dve", "pool"):
            self.newsem(e)

    def newsem(self, key):
        self.sems[key] = self.es.enter_context(self.nc.semaphore("s_" + str(key)))
        self.cnt[key] = 0

    def _deps(self, eng, R, W):
        deps = {}

        def add(tok):
            k, v = tok
            if deps.get(k, 0) < v:
                deps[k] = v

        for c in R:
            t = self.lastw.get(c)
            if t is not None:
                add(t)
        for c in W:
            t = self.lastw.get(c)
            if t is not None:
                add(t)
            rd = self.readers.get(c)
            if rd:
                for k, v in rd.items():
                    add((k, v))
        waits = []
        wd = self.waited[eng]
        for k, v in deps.items():
            if k == "pe" and eng == "pe":
                continue
            if wd.get(k, 0) < v:
                wd[k] = v
                waits.append((k, v))
        return waits

    def _record(self, tok, R, W):
        for c in W:
            self.lastw[c] = tok
            self.readers[c] = {}
        k, v = tok
        for c in R:
            rd = self.readers.setdefault(c, {})
            if rd.get(k, 0) < v:
                rd[k] = v

    def op(self, eng, fn, R, W, inc=True):
        waits = self._deps(eng, R, W)
        if inc:
            self.cnt[eng] += 1
            tok = (eng, self.cnt[eng])
        else:
            tok = (eng, self.cnt[eng] + 1)
        self._record(tok, R, W)
        self.q[eng].append((waits, fn, (eng, 1) if inc else None))
        return tok

    def dma(self, queue, semkey, pairs, **kw):
        if semkey not in self.sems:
            self.newsem(semkey)
        R, W = [], []
        for o, i in pairs:
            if isinstance(o, O):
                W += o.cells
            if isinstance(i, O):
                R += i.cells
        waits = self._deps(queue, R, W)
        self.cnt[semkey] += 16 * len(pairs)
        tok = (semkey, self.cnt[semkey])
        self._record(tok, R, W)
        first = True
        for o, i in pairs:
            oa = o.ap if isinstance(o, O) else o
            ia = i.ap if isinstance(i, O) else i
            self.q[queue].append((waits if first else [], (lambda e, oa=oa, ia=ia: e.dma_start(out=oa, in_=ia, **kw)), (semkey, 16)))
            first = False
        return tok

    def final_wait(self, eng, keys):
        waits = [(k, self.cnt[k]) for k in keys if self.cnt[k] > 0]
        self.q[eng].append((waits, None, None))

    def emit(self, block):
        sems = self.sems

        def run(e, lst):
            for waits, fn, inc in lst:
                for k, v in waits:
                    e.wait_ge(sems[k], v)
                if fn is not None:
                    ins = fn(e)
                    if inc is not None:
                        ins.then_inc(sems[inc[0]], inc[1])

        q = self.q
        block.tensor(lambda e: run(e, q["pe"]))
        block.scalar(lambda e: run(e, q["act"]))
        block.vector(lambda e: run(e, q["dve"]))
        block.gpsimd(lambda e: run(e, q["pool"]))
        block.sync(lambda e: run(e, q["sp"]))


def build_program(debug=False, nblocks=NB, phases=("ffn1", "mix", "ffn2", "ple")):
    nc = bass.Bass("TRN2", target_bir_lowering=False)

    def din(name, shape):
        return nc.dram_tensor(name, list(shape), F32, kind="ExternalInput").ap()

    x_d = din("x", (D, S))
    p_d = din("p", (PLE, S))
    w = {}
    for pre in ("ffn1", "ffn2"):
        w[pre + "_wg"] = din(pre + "_wg", (F // 128, 128, KC, 128))
        w[pre + "_wu"] = din(pre + "_wu", (F // 128, 128, KC, 128))
        w[pre + "_wd"] = din(pre + "_wd", (D // 128, 128, F // 128, 128))
    w_in = din("w_in", ((INW - 8) // 128, 128, KC, 128))
    w_fa = din("w_fa", (128, KC, 8))
    w_pf = din("w_pf", (D // 128, 128, NH, 128))
    w_ph = din("w_ph", (D // 128, 128, NH, 128))
    w_o = din("w_o", (D // 128, 128, KC, 128))
    w_pg = din("w_pg", (D // 128, 128, KC, 128))
    w_pp = din("w_pp", (D // 128, 128, 2, 128))
    gain_names = ["ffn1_pre", "ffn1_post", "mix_pre", "mix_post", "ffn2_pre", "ffn2_post", "ple_pre", "ple_post"]
    gains_d = din("gains", (128, 8, KC))
    fbias_d = din("fbias", (NH, 1))
    lbl_d = din("lbl", (128, 2, NH))
    gn_d = din("gn", (DH, 1))
    out_d = nc.dram_tensor("out", [D, S], F32, kind="ExternalOutput").ap()
    dbg_d = None
    if debug:
        dbg_d = nc.dram_tensor("dbg", [8, 128, KC * TB], F32, kind="ExternalOutput").ap()

    with ExitStack() as es:
        sc = Sched(nc, es)
        nb_ctr = [0]

        def sbt(shape, dt, name=None):
            nb_ctr[0] += 1
            return es.enter_context(nc.sbuf_tensor(name or ("t%d" % nb_ctr[0]), list(shape), dt))

        def mkbuf(fshape, dt, name, parts=128, cell=CELL):
            t = sbt([parts] + list(fshape), dt, name)
            return Buf(t[:], fshape, 4 if dt == F32 else 2, name, 0, cell)

        hT = mkbuf((KC, TB), F32, "hT")
        KT = mkbuf((NH, S), BF16, "KT")
        VV = mkbuf((S // 128, NH * DH), BF16, "VV")
        wslots = [mkbuf((KC, 128), BF16, "wslot%d" % i) for i in range(NSLOT)]
        ARENA = 68 * 1024
        arena_t = sbt([128, ARENA // 4], F32, "arena")

        def av(off, fshape, dt, parts=128):
            es_ = 4 if dt == F32 else 2
            n = int(np.prod(fshape))
            nb = n * es_
            assert off % 4 == 0 and nb % 4 == 0 and off + nb <= ARENA, (off, nb)
            ap = arena_t[0:parts, off // 4:(off + nb) // 4]
            if dt != F32:
                ap = ap.bitcast(dt)
            if len(fshape) == 2:
                ap = ap.rearrange("p (a b) -> p a b", a=fshape[0])
            elif len(fshape) == 3:
                ap = ap.rearrange("p (a b c) -> p a b c", a=fshape[0], b=fshape[1])
            return Buf(ap, fshape, es_, "arena", off)

        K = 1024
        uT = av(0, (KC, TB), BF16)
        aT = av(16 * K, (12, TB), BF16)
        sgb = [av(28 * K + i * K, (TB,), BF16) for i in range(2)]
        sqb = [av(30 * K + i * K, (TB,), BF16) for i in range(2)]
        yb = av(32 * K, (KC, TB), F32)
        rstd = av(64 * K, (TB,), F32)
        tmpf = av(66 * K, (TB,), F32)
        sq_alt = [av(i * K, (TB,), BF16) for i in range(2)]
        QT = av(16 * K, (NH, TB), BF16)
        PTb = [av(24 * K + i * K, (TB,), BF16) for i in range(4)]
        cT = av(36 * K, (TB,), F32)
        spb = av(38 * K, (TB,), F32)
        recb = av(40 * K, (TB,), F32)
        chb = [av(42 * K + i * K, (TB,), BF16) for i in range(2)]
        yfox = av(56 * K, (NH, TB), BF16)
        QtT = av(16 * K, (4, TB), BF16)
        KtT = av(20 * K, (4, TB), BF16)
        Ktok = av(24 * K, (4, 4 * DH), BF16)
        Vh = av(28 * K, (4, 4 * DH), BF16)
        gsb = av(32 * K, (4, TB), BF16)
        tq = [av(36 * K + i * 2 * K, (TB,), F32) for i in range(6)]
        ohg = av(48 * K, (NH, TB), BF16)
        merged = av(16 * K, (KC, TB), BF16)
        sga = [av(32 * K + i * 2 * K, (TB,), F32) for i in range(2)]
        sgbb = [av(36 * K + i * 2 * K, (TB,), F32) for i in range(2)]
        m1 = av(40 * K, (TB,), F32)
        m2 = av(42 * K, (TB,), F32)
        sgp = [av(24 * K + i * 2 * K, (TB,), F32) for i in range(2)]

        ident = mkbuf((128,), F32, "ident")
        identb = mkbuf((128,), BF16, "identb")
        onesb = mkbuf((128,), BF16, "onesb")
        tri = mkbuf((128,), BF16, "tri")
        trif = mkbuf((128,), F32, "trif")
        hmask = mkbuf((128,), F32, "hmask")
        d0m = mkbuf((TB,), F32, "d0m")
        ones8 = mkbuf((128,), F32, "ones8", parts=8)
        onesf = mkbuf((TB,), F32, "onesf", parts=8)
        negI8 = mkbuf((8,), F32, "negI8", parts=8)
        oh8 = mkbuf((8,), F32, "oh8", parts=8)
        gains = mkbuf((8, KC), F32, "gains_sb")
        gsc = mkbuf((8, KC), F32, "gains_sc")
        lb = mkbuf((NH,), F32, "lb")
        oml = mkbuf((NH,), F32, "oml")
        ltmp = mkbuf((2, NH), F32, "ltmp")
        negfb = mkbuf((1,), F32, "negfb", parts=8)
        fbt = mkbuf((1,), F32, "fbt", parts=8)
        gn = mkbuf((1,), F32, "gn_sb")
        Sst = mkbuf((NH, DH), F32, "Sst", cell=512)
        Sbf = mkbuf((NH, DH), BF16, "Sbf", cell=256)
        negc = mkbuf((S // 128, NH), F32, "negc")
        carry = mkbuf((1,), F32, "carry", parts=8)
        PTm = mkbuf((4, 128), BF16, "PTm", cell=256)
        pT = mkbuf((2, TB), BF16, "pT")
        stmp = mkbuf((4, 128), F32, "stmp", cell=512)

        PS = []
        for i in range(8):
            t = es.enter_context(nc.psum_tensor("ps%d" % i, [128, TB], F32))
            PS.append(Buf(t[:], (TB,), 4, "ps%d" % i, 0, 2048))

        def ap_of(x):
            return x.ap if isinstance(x, O) else x

        def cells_of(*xs):
            r = []
            for x in xs:
                if isinstance(x, O):
                    r += x.cells
            return r

        def mm(out, lhsT, rhs, start=True, stop=True, inc=True, skip=False):
            R = lhsT.cells + rhs.cells
            sc.op("pe", lambda e: e.matmul(out.ap, lhsT.ap, rhs.ap, start=start, stop=stop, skip_group_check=skip),
                  R, out.cells, inc=inc)

        def act(out, in_, func, bias=None, scale=None, accum=None, eng="act"):
            kw = {}
            if bias is not None:
                kw["bias"] = ap_of(bias)
            if scale is not None:
                kw["scale"] = ap_of(scale)
            if accum is not None:
                kw["accum_out"] = accum.ap
            sc.op("act", lambda e: e.activation(out=out.ap, in_=in_.ap, func=func, **kw),
                  cells_of(in_, bias, scale), cells_of(out, accum))

        def tt(eng, out, a, b, op):
            sc.op(eng, lambda e: e.tensor_tensor(out=out.ap, in0=a.ap, in1=b.ap, op=op), cells_of(a, b), out.cells)

        def ts(eng, out, a, s1, s2, op0, op1=None):
            if op1 is None:
                sc.op(eng, lambda e: e.tensor_scalar(out=out.ap, in0=a.ap, scalar1=ap_of(s1), scalar2=None, op0=op0),
                      cells_of(a, s1), out.cells)
            else:
                sc.op(eng, lambda e: e.tensor_scalar(out=out.ap, in0=a.ap, scalar1=ap_of(s1), scalar2=ap_of(s2), op0=op0, op1=op1),
                      cells_of(a, s1, s2), out.cells)

        def stt(eng, out, a, scalar, b, op0, op1):
            sc.op(eng, lambda e: e.scalar_tensor_tensor(out=out.ap, in0=a.ap, scalar=ap_of(scalar), in1=b.ap, op0=op0, op1=op1),
                  cells_of(a, scalar, b), out.cells)

        def scan(eng, out, d0, d1, init, op0, op1):
            sc.op(eng, lambda e: e.tensor_tensor_scan(out.ap, d0.ap, d1.ap, ap_of(init), op0, op1),
                  cells_of(d0, d1, init), out.cells)

        def cp(eng, out, in_):
            if eng == "act":
                sc.op("act", lambda e: e.copy(out=out.ap, in_=in_.ap), in_.cells, out.cells)
            else:
                sc.op(eng, lambda e: e.tensor_copy(out=out.ap, in_=in_.ap), in_.cells, out.cells)

        def act_lnexp_tables():
            pass

        def memset(eng, out, val):
            sc.op(eng, lambda e: e.memset(out.ap, val), [], out.cells)

        def recip(out, in_):
            sc.op("dve", lambda e: e.reciprocal(out=out.ap, in_=in_.ap), in_.cells, out.cells)

        def transpose(out, in_):
            sc.op("pe", lambda e: e.transpose(out.ap, in_.ap, ident.ap[:, :]), in_.cells + ident.s().cells, out.cells)

        wctr = [0]

        prefetched = {}

        def wprefetch(specs):
            for sp_ in specs:
                key = (id(sp_[0]),) + tuple(sp_[1:])
                prefetched.setdefault(key, []).append(wload(*sp_, _nopf=True))

        def wload(tiled, t, k0=0, nk=KC, cols=128, _nopf=False):
            if not _nopf:
                key = (id(tiled), t, k0, nk, cols)
                if prefetched.get(key):
                    return prefetched[key].pop(0)
            i = wctr[0] % NSLOT
            wctr[0] += 1
            slot = wslots[i]
            src = tiled[t] if t is not None else tiled
            sc.dma("pool", "wsem%d" % i, [(slot.s(slice(0, nk), slice(0, cols)), src[:, k0:k0 + nk, :])])
            return slot

        memset("dve", ident.s(), 0.0)
        sc.op("pool", lambda e: e.affine_select(out=ident.ap[:, :], in_=ident.ap[:, :], pattern=[[-1, 128]],
                                                compare_op=ALU.not_equal, fill=1.0, base=0, channel_multiplier=1),
              ident.s().cells, ident.s().cells)
        cp("dve", identb.s(), ident.s())
        memset("dve", onesb.s(), 1.0)
        memset("dve", trif.s(), 0.0)
        sc.op("pool", lambda e: e.affine_select(out=trif.ap[:, :], in_=trif.ap[:, :], pattern=[[1, 128]],
                                                compare_op=ALU.is_ge, fill=-30000.0, base=0, channel_multiplier=-1),
              trif.s().cells, trif.s().cells)
        cp("dve", tri.s(), trif.s())
        memset("dve", hmask.s(), 1.0)
        sc.op("pool", lambda e: e.affine_select(out=hmask.ap[:, :], in_=hmask.ap[:, :], pattern=[[1, 128]],
                                                compare_op=ALU.is_ge, fill=0.0, base=0, channel_multiplier=-1),
              hmask.s().cells, hmask.s().cells)
        memset("dve", hmask.s(slice(64, 128), p=(0, 64)), 0.0)
        memset("dve", d0m.s(), 1.0)
        sc.op("dve", lambda e: e.memset(d0m.ap.rearrange("p (c t) -> p c t", t=64)[:, :, 0:1], 0.0), [], d0m.s().cells)
        memset("dve", ones8.s(), 1.0)
        memset("dve", onesf.s(), 1.0)
        memset("dve", negI8.s(), 0.0)
        sc.op("pool", lambda e: e.affine_select(out=negI8.ap[:, :], in_=negI8.ap[:, :], pattern=[[-1, 8]],
                                                compare_op=ALU.not_equal, fill=-1.0, base=0, channel_multiplier=1),
              negI8.s().cells, negI8.s().cells)
        ts("dve", oh8.s(), negI8.s(), -1.0, None, ALU.mult)
        memset("dve", Sst.s(), 0.0)
        memset("dve", Sbf.s(), 0.0)
        memset("dve", carry.s(), 0.0)
        sc.dma("sp", "small", [
            (gains.s(), gains_d),
            (ltmp.s(), lbl_d),
            (O(fbt.ap[0:8, :], fbt.s().cells), fbias_d),
            (gn.s(), gn_d),
        ])
        for gi, gname in enumerate(gain_names):
            scl = 0.5 if gname in ("ffn1_post", "ffn2_post") else 1.0
            ts("dve", gsc.s(gi), gains.s(gi), scl, None, ALU.mult)
        tt("dve", lb.s(), ltmp.s(1), ltmp.s(0), ALU.subtract)
        act(lb.s(), lb.s(), AF.Exp)
        ts("dve", oml.s(), lb.s(), 1.0, None, ALU.add)
        recip(lb.s(), oml.s())
        ts("dve", oml.s(), lb.s(), -1.0, 1.0, ALU.mult, ALU.add)
        ts("dve", negfb.s(), fbt.s(), -1.0, None, ALU.mult)

        GI = {n: i for i, n in enumerate(gain_names)}

        def rstd_from(ps_ss, n):
            act_lnexp_tables()
            act(rstd.s(), ps_ss, AF.Ln, bias=EPS, scale=1.0 / n)
            act(rstd.s(), rstd.s(), AF.Exp, scale=-0.5)

        def prenorm(gname):
            gi = GI[gname]
            pss = PS[7].s()
            for dc in range(KC):
                sq = sqb[dc % 2]
                act(sq.s(), hT.s(dc), AF.Square)
                mm(pss, onesb.s(), sq.s(), start=(dc == 0), stop=(dc == KC - 1))
            rstd_from(pss, D)
            for dc in range(KC):
                stt("dve", uT.s(dc), hT.s(dc), gains.s(gi, slice(dc, dc + 1)), rstd.s(), ALU.mult, ALU.mult)

        def ss_sq(dc, sqs=None):
            sq = (sqs or sqb)[dc % 2]
            act(sq.s(), yb.s(dc), AF.Square)

        def ss_mm(dc, sqs=None):
            sq = (sqs or sqb)[dc % 2]
            mm(PS[7].s(), onesb.s(), sq.s(), start=(dc == 0), stop=(dc == KC - 1))

        def ss_accum(dc, sqs=None, defer=2):
            if dc - defer >= 0:
                ss_mm(dc - defer, sqs)
            ss_sq(dc, sqs)
            if dc == KC - 1:
                for d2 in range(max(0, KC - defer), KC):
                    ss_mm(d2, sqs)

        def postnorm(gname, next_specs=()):
            gi = GI[gname]
            wprefetch(next_specs)
            rstd_from(PS[7].s(), D)
            for d2 in range(0, KC, 2):
                for dc in (d2, d2 + 1):
                    stt("dve", yb.s(dc), yb.s(dc), gsc.s(gi, slice(dc, dc + 1)), rstd.s(), ALU.mult, ALU.mult)
                tt("dve", hT.s(slice(d2, d2 + 2)), hT.s(slice(d2, d2 + 2)), yb.s(slice(d2, d2 + 2)), ALU.add)

        def proj_fm(ps, wslot, col0, ncols, rhs_buf, nk):
            for kc in range(nk):
                mm(O(ps.ap[0:ncols, :], ps.cells), wslot.s(kc, slice(col0, col0 + ncols)), rhs_buf.s(kc),
                   start=(kc == 0), stop=(kc == nk - 1), inc=(kc == nk - 1))

        def ffn(pre, b, next_specs=()):
            wg, wu, wd = w[pre + "_wg"], w[pre + "_wu"], w[pre + "_wd"]
            prenorm(pre + "_pre")
            f0 = 0
            pi = 0
            for g, gsz in enumerate(FGROUPS):
                fl_start = 0
                if g == 0:
                    sl4 = [wload(wg, 0), wload(wu, 0), wload(wg, 1), wload(wu, 1)]
                    for kc in range(KC):
                        for i4 in range(4):
                            mm(PS[i4].s(), sl4[i4].s(kc), uT.s(kc), start=(kc == 0), stop=(kc == KC - 1), inc=(kc == KC - 1))
                    for fl in range(2):
                        sgt = sgb[fl % 2]
                        act(sgt.s(), PS[2 * fl].s(), AF.Silu)
                        tt("dve", aT.s(fl), PS[2 * fl + 1].s(), sgt.s(), ALU.mult)
                    fl_start = 2
                for fl in range(fl_start, gsz):
                    sg_ = wload(wg, f0 + fl)
                    su_ = wload(wu, f0 + fl)
                    pg = PS[(pi % 2) * 2]
                    pu = PS[(pi % 2) * 2 + 1]
                    pi += 1
                    proj_fm(pg.s(), sg_, 0, 128, uT, KC)
                    proj_fm(pu.s(), su_, 0, 128, uT, KC)
                    sgt = sgb[fl % 2]
                    act(sgt.s(), pg.s(), AF.Silu)
                    tt("dve", aT.s(fl), pu.s(), sgt.s(), ALU.mult)
                for dc in range(KC):
                    sd_ = wload(wd, dc, f0, gsz)
                    py = PS[4 + (dc % 2)]
                    for fl in range(gsz):
                        mm(py.s(), sd_.s(fl), aT.s(fl), start=(fl == 0), stop=(fl == gsz - 1), inc=(fl == gsz - 1))
                    if g == 0:
                        cp("act", yb.s(dc), py.s())
                    else:
                        tt("dve", yb.s(dc), yb.s(dc), py.s(), ALU.add)
                    if g == len(FGROUPS) - 1:
                        ss_accum(dc)
                f0 += gsz
            postnorm(pre + "_post", next_specs)

        def load_x(b):
            t0 = b * TB
            for g4 in range(4):
                src = x_d[g4 * 512:(g4 + 1) * 512, t0:t0 + TB].rearrange("(c p) t -> p c t", p=128)
                sc.dma("sp", "xs%d" % g4, [(hT.s(slice(g4 * 4, g4 * 4 + 4)), src)])

        def store_out(b):
            t0 = b * TB
            for g4 in range(4):
                dst = out_d[g4 * 512:(g4 + 1) * 512, t0:t0 + TB].rearrange("(c p) t -> p c t", p=128)
                sc.dma("sp", "os%d" % g4, [(dst, hT.s(slice(g4 * 4, g4 * 4 + 4)))])

        def dump(slot):
            if debug:
                sc.dma("sp", "dbg", [(dbg_d[slot], O(hT.ap.rearrange("p a b -> p (a b)"), hT.s().cells))])

        def mixer(b, next_specs=()):
            t0 = b * TB
            prenorm("mix_pre")
            sfa = wload(w_fa, None, 0, KC, 8)
            sk3 = [wload(w_in, M_KA + h) for h in range(3)]
            pf = PS[6]
            for kc in range(KC):
                mm(O(pf.ap[0:8, :], pf.cells), sfa.s(kc, slice(0, 8)), uT.s(kc), start=(kc == 0), stop=(kc == KC - 1), inc=(kc == KC - 1))
                for h in range(3):
                    mm(PS[h].s(), sk3[h].s(kc), uT.s(kc), start=(kc == 0), stop=(kc == KC - 1), inc=(kc == KC - 1))
            for h in range(3):
                cp("act", KT.s(h, slice(t0, t0 + TB)), PS[h].s())
            sp8 = O(spb.ap[0:8, :], spb.s().cells)
            cT8 = O(cT.ap[0:8, :], cT.s().cells)
            act_lnexp_tables()
            act(sp8, O(pf.ap[0:8, :], pf.cells), AF.Exp, bias=O(negfb.ap[0:8, :], negfb.s().cells), scale=-1.0)
            act(sp8, sp8, AF.Ln, bias=1.0, scale=1.0)
            scan("dve", cT8, O(onesf.ap[0:8, :], onesf.s().cells), sp8, O(carry.ap[0:8, :], carry.s().cells), ALU.mult, ALU.subtract)
            cp("dve", O(carry.ap[0:8, :], carry.s().cells), O(cT.ap[0:8, TB - 1:TB], cT.s().cells))
            for j in range(4):
                pn = PS[4 + (j % 2)]
                mm(O(pn.ap[:, 0:8], pn.cells), O(cT.ap[0:8, j * 128:(j + 1) * 128], cT.s().cells),
                   O(negI8.ap[0:8, :], negI8.s().cells))
                cp("dve", negc.s(b * 4 + j), O(pn.ap[:, 0:8], pn.cells))
            pi = 3
            for h in range(3, NH):
                sk_ = wload(w_in, M_KA + h)
                ps = PS[pi % 4]
                pi += 1
                proj_fm(ps.s(), sk_, 0, 128, uT, KC)
                cp("act", KT.s(h, slice(t0, t0 + TB)), ps.s())
            for h in range(NH):
                sq_ = wload(w_in, M_QA + h)
                ps = PS[pi % 4]
                pi += 1
                proj_fm(ps.s(), sq_, 0, 128, uT, KC)
                act(QT.s(h), ps.s(), AF.Copy, scale=float(DH ** -0.5))
            for h in range(NH):
                sv_ = wload(w_in, M_VA + h)
                for j in range(4):
                    ps = PS[pi % 4]
                    pi += 1
                    po = O(ps.ap[:, 0:128], ps.cells)
                    for kc in range(KC):
                        mm(po, uT.s(kc, slice(j * 128, (j + 1) * 128)), sv_.s(kc), start=(kc == 0), stop=(kc == KC - 1), inc=(kc == KC - 1))
                    cp("dve" if j % 2 else "act", VV.s(b * 4 + j, slice(h * 128, (h + 1) * 128)), po)
            nkt = 4 * (b + 1)
            LA = 3
            SBK = [PS[0], PS[1], PS[4], PS[5]]

            def make_ch(h):
                ch = chb[h % 2]
                ts("dve", O(ch.ap[0:8, :], ch.cells), cT8, O(oh8.ap[0:8, h:h + 1], oh8.cells), None, ALU.mult)

            make_ch(0)
            for h in range(NH):
                chh = chb[h % 2]
                po = PS[2] if h % 2 == 0 else PS[6]
                pd = PS[3] if h % 2 == 0 else PS[7]

                def scores(j):
                    r = j - 4 * b
                    col0 = 128 * r if r > 0 else 0
                    cs = slice(col0, TB)
                    ps = SBK[j % 4]
                    kt = KT.s(h, slice(j * 128, (j + 1) * 128))
                    diag = r >= 0
                    mm(ps.s(cs), kt, QT.s(h, cs), start=True, stop=False, inc=False, skip=True)
                    mm(ps.s(cs), O(onesb.ap[0:8, :], onesb.cells), O(chh.ap[0:8, cs], chh.cells),
                       start=False, stop=(not diag), inc=(not diag), skip=True)
                    if diag:
                        mm(ps.s(slice(col0, col0 + 128)), identb.s(), tri.s(), start=False, stop=True, inc=True, skip=True)
                    return col0

                col0s = {}
                for j in range(min(LA, nkt)):
                    col0s[j] = scores(j)
                if h + 1 < NH:
                    make_ch(h + 1)
                for j in range(nkt):
                    if j + LA < nkt:
                        col0s[j + LA] = scores(j + LA)
                    col0 = col0s[j]
                    cs = slice(col0, TB)
                    ps = SBK[j % 4]
                    PT_ = PTb[j % 4]
                    act(PT_.s(cs), ps.s(cs), AF.Exp, bias=negc.s(j, slice(h, h + 1)))
                    last = (j == nkt - 1)
                    mm(po.s(cs), VV.s(j, slice(h * 128, (h + 1) * 128)), PT_.s(cs), start=(j == 0), stop=last, inc=True, skip=True)
                    mm(pd.s(cs), onesb.s(), PT_.s(cs), start=(j == 0), stop=last, inc=True, skip=True)
                recip(recb.s(), pd.s())
                tt("dve", yfox.s(h), po.s(), recb.s(), ALU.mult)

            for hh in range(2):
                pi = 0
                sets = [tq[0:4], [tq[4], tq[5], rstd, tmpf]]
                for pr2 in range(2):
                    hls = (2 * pr2, 2 * pr2 + 1)
                    for i2, hl in enumerate(hls):
                        h = hh * 4 + hl
                        sq_ = wload(w_in, M_QB + h)
                        sf_ = wload(w_in, M_FB + h)
                        proj_fm(PS[i2].s(), sq_, 0, 128, uT, KC)
                        proj_fm(PS[2 + i2].s(), sf_, 0, 128, uT, KC)
                    for i2, hl in enumerate(hls):
                        tA, tB, tC, tD = sets[i2]
                        act(tA.s(), PS[2 + i2].s(), AF.Sigmoid)
                        act(tD.s(), PS[i2].s(), AF.Sigmoid)
                    for i2, hl in enumerate(hls):
                        h = hh * 4 + hl
                        tA, tB, tC, tD = sets[i2]
                        ts("dve", tA.s(), tA.s(), oml.s(slice(h, h + 1)), lb.s(slice(h, h + 1)), ALU.mult, ALU.add)
                        tt("dve", tD.s(), PS[i2].s(), tD.s(), ALU.mult)
                    act_lnexp_tables()
                    for i2, hl in enumerate(hls):
                        tA, tB, tC, tD = sets[i2]
                        act(tB.s(), tA.s(), AF.Ln)
                        scan("dve", tC.s(), d0m.s(), tB.s(), 0.0, ALU.mult, ALU.add)
                        ts("dve", tA.s(), tA.s(), -1.0, 1.0, ALU.mult, ALU.add)
                        act(tB.s(), tC.s(), AF.Exp)
                        sc.op("dve", lambda e, hl=hl, tB=tB: e.tensor_copy(
                            out=stmp.ap[:, hl, 0:8], in_=tB.ap.rearrange("p (c t) -> p c t", t=64)[:, :, 63]),
                            tB.s().cells, stmp.s(hl).cells)
                        tt("dve", QtT.s(hl), tD.s(), tB.s(), ALU.mult)
                        act(tB.s(), tC.s(), AF.Exp, scale=-1.0)
                        tt("dve", KtT.s(hl), tA.s(), tB.s(), ALU.mult)
                for hl in range(4):
                    si_ = wload(w_in, M_IB + hh * 4 + hl)
                    for j in range(4):
                        ps = PS[pi % 2]
                        pi += 1
                        po = O(ps.ap[:, 0:128], ps.cells)
                        for kc in range(KC):
                            mm(po, uT.s(kc, slice(j * 128, (j + 1) * 128)), si_.s(kc), start=(kc == 0), stop=(kc == KC - 1), inc=(kc == KC - 1))
                        cp("dve" if j % 2 else "act", Vh.s(j, slice(hl * 128, (hl + 1) * 128)), po)
                for hl in range(4):
                    for j in range(4):
                        ps = PS[pi % 2]
                        pi += 1
                        po = O(ps.ap[:, 0:128], ps.cells)
                        mm(po, KtT.s(hl, slice(j * 128, (j + 1) * 128)), identb.s())
                        cp("dve" if j % 2 else "act", Ktok.s(j, slice(hl * 128, (hl + 1) * 128)), po)
                S4 = Sst.s(slice(hh * 4, hh * 4 + 4))
                Sb4 = Sbf.s(slice(hh * 4, hh * 4 + 4))
                hm_b = O(hmask.ap.unsqueeze(1).to_broadcast([128, 4, 128]), hmask.cells)
                gslot = [None]
                for j in range(4):
                    bs = PS[6]
                    for hl in range(4):
                        mm(bs.s(slice(hl * 128, (hl + 1) * 128)), KtT.s(hl, slice(j * 128, (j + 1) * 128)),
                           QtT.s(hl, slice(j * 128, (j + 1) * 128)))
                    tt("dve", PTm.s(), O(bs.ap.rearrange("p (a t) -> p a t", a=4), bs.cells), hm_b, ALU.mult)
                    for c in range(2):
                        pr = (0, 64) if c == 0 else (64, 128)
                        bd = PS[c]
                        for hl in range(4):
                            mm(bd.s(slice(hl * 128, (hl + 1) * 128)), Ktok.s(j, slice(hl * 128, (hl + 1) * 128), p=pr),
                               Vh.s(j, slice(hl * 128, (hl + 1) * 128), p=pr))
                        st_ = j * 2 + c
                        hg, part = st_ // 2, st_ % 2
                        if part == 0:
                            gslot[0] = wload(w_in, M_GB + hh * 4 + hg)
                        for kc in range(part * 8, part * 8 + 8):
                            mm(PS[7].s(), gslot[0].s(kc), uT.s(kc), start=(kc == 0), stop=(kc == KC - 1), inc=(kc == KC - 1))
                        if part == 1:
                            act(gsb.s(hg), PS[7].s(), AF.Silu)
                        cols = slice(j * 128 + c * 64, j * 128 + c * 64 + 64)
                        pcols = slice(c * 64, c * 64 + 64)
                        for hl in range(4):
                            h = hh * 4 + hl
                            pov = PS[2 + hl].s(cols)
                            mm(pov, Vh.s(j, slice(hl * 128, (hl + 1) * 128), p=pr), PTm.s(hl, pcols, p=pr), start=True, stop=False, inc=False)
                            mm(pov, Sbf.s(h), QtT.s(hl, cols), start=False, stop=True, inc=True)
                        ci = j * 2 + c
                        el_b = O(stmp.ap[:, :, ci:ci + 1].to_broadcast([128, 4, 128]), stmp.cells)
                        tt("dve", S4, S4, O(bd.ap.rearrange("p (a t) -> p a t", a=4), bd.cells), ALU.add)
                        tt("dve", Sb4, S4, el_b, ALU.mult)
                        tt("dve", S4, S4, el_b, ALU.mult)
                act_lnexp_tables()
                for hl in range(4):
                    h = hh * 4 + hl
                    pov = PS[2 + hl]
                    osb = tq[0] if hl % 2 == 0 else tq[2]
                    rs = tq[1] if hl % 2 == 0 else tq[3]
                    sq = sqb[hl % 2]
                    act(sq.s(), pov.s(), AF.Square)
                    pss = PS[hl % 2]
                    mm(pss.s(), onesb.s(), sq.s())
                    act(rs.s(), pss.s(), AF.Ln, bias=EPS, scale=1.0 / DH)
                    act(rs.s(), rs.s(), AF.Exp, scale=-0.5)
                    stt("dve", osb.s(), pov.s(), gn.s(), rs.s(), ALU.mult, ALU.mult)
                    tt("dve", ohg.s(h), osb.s(), gsb.s(hl), ALU.mult)

            for dc in range(KC):
                q = dc % 2
                sa_ = wload(w_in, M_GA + dc)
                spf = wload(w_pf, dc, 0, NH)
                p1, p2 = PS[2 * q], PS[2 * q + 1]
                proj_fm(p1.s(), sa_, 0, 128, uT, KC)
                proj_fm(p2.s(), spf, 0, 128, yfox, NH)
                ga = sga[q]
                act(ga.s(), p1.s(), AF.Sigmoid)
                mq = m1 if q == 0 else m2
                tt("dve", mq.s(), p2.s(), ga.s(), ALU.mult)
                sb_ = wload(w_in, M_GBT + dc)
                sph = wload(w_ph, dc, 0, NH)
                p3, p4 = PS[4 + 2 * q], PS[5 + 2 * q]
                proj_fm(p3.s(), sb_, 0, 128, uT, KC)
                proj_fm(p4.s(), sph, 0, 128, ohg, NH)
                gb_ = sgbb[q]
                act(gb_.s(), p3.s(), AF.Sigmoid)
                tt("dve", gb_.s(), p4.s(), gb_.s(), ALU.mult)
                tt("dve", merged.s(dc), mq.s(), gb_.s(), ALU.add)
            if "nowout" in phases:
                return
            for dc in range(KC):
                so_ = wload(w_o, dc)
                py = PS[dc % 4]
                proj_fm(py.s(), so_, 0, 128, merged, KC)
                cp("act", yb.s(dc), py.s())
                ss_accum(dc, sq_alt)
            postnorm("mix_post", next_specs)

        def ple(b, next_specs=()):
            t0 = b * TB
            prenorm("ple_pre")
            for dc in range(KC):
                sg_ = wload(w_pg, dc)
                sp_ = wload(w_pp, dc, 0, 2)
                p1 = PS[(dc % 2) * 2]
                p2 = PS[(dc % 2) * 2 + 1]
                proj_fm(p1.s(), sg_, 0, 128, uT, KC)
                proj_fm(p2.s(), sp_, 0, 128, pT, 2)
                sg1 = sgp[dc % 2]
                act(sg1.s(), p1.s(), AF.Sigmoid)
                tt("dve", yb.s(dc), p2.s(), sg1.s(), ALU.mult)
                ss_accum(dc)
            postnorm("ple_post", next_specs)

        for b in range(nblocks):
            load_x(b)
            sc.dma("pool", "pld", [(pT.s(), p_d[:, b * TB:(b + 1) * TB].rearrange("(k p) t -> p k t", p=128))])
            if "ffn1" in phases:
                ffn("ffn1", b, [(w_fa, None, 0, KC, 8)] + [(w_in, M_KA + h_, 0, KC, 128) for h_ in range(3)])
            if b == 0:
                dump(0)
            if "mix" in phases:
                mixer(b, [(w["ffn2_wg"], 0, 0, KC, 128), (w["ffn2_wu"], 0, 0, KC, 128), (w["ffn2_wg"], 1, 0, KC, 128), (w["ffn2_wu"], 1, 0, KC, 128)])
            if b == 0:
                dump(1)
            if "ffn2" in phases:
                ffn("ffn2", b, [(w_pg, 0, 0, KC, 128), (w_pp, 0, 0, 2, 128), (w_pg, 1, 0, KC, 128), (w_pp, 1, 0, 2, 128)])
            if b == 0:
                dump(2)
            if "ple" in phases:
                ple(b, [(w["ffn1_wg"], 0, 0, KC, 128), (w["ffn1_wu"], 0, 0, KC, 128), (w["ffn1_wg"], 1, 0, KC, 128), (w["ffn1_wu"], 1, 0, KC, 128)] if b + 1 < nblocks else [])
            if b == 0:
                dump(3)
            store_out(b)
        fin = ["os0", "os1", "os2", "os3"] + (["dbg"] if debug else [])
        sc.final_wait("sp", fin)

        block = es.enter_context(nc.Block())
        sc.emit(block)
    return nc


_NC_CACHE = {}


def _get_nc(**kw):
    key = tuple(sorted((k, str(v)) for k, v in kw.items()))
    if key not in _NC_CACHE:
        _NC_CACHE[key] = build_program(**kw)
    return _NC_CACHE[key]


def _tile_w(W, cols=128):
    K_, N_ = W.shape
    return np.ascontiguousarray(W.reshape(K_ // 128, 128, N_ // cols, cols).transpose(2, 1, 0, 3))


def make_in_maps(inputs, ncores=8):
    f = lambda a: np.ascontiguousarray(np.asarray(a, dtype=np.float32))
    win = f(inputs["mix_w_in"][0])
    win_main = np.concatenate([win[:, :C_FA], win[:, C_FA + 8:]], axis=1)
    gnames = ["ffn1_pre", "ffn1_post", "mix_pre", "mix_post", "ffn2_pre", "ffn2_post", "ple_pre", "ple_post"]
    gains = np.stack([np.asarray(inputs[n + "_g"], dtype=np.float32)[0] for n in gnames], axis=0)
    common = {
        "ffn1_wg": _tile_w(f(inputs["ffn1_w_gate"][0])), "ffn1_wu": _tile_w(f(inputs["ffn1_w_up"][0])),
        "ffn1_wd": _tile_w(f(inputs["ffn1_w_down"][0])),
        "ffn2_wg": _tile_w(f(inputs["ffn2_w_gate"][0])), "ffn2_wu": _tile_w(f(inputs["ffn2_w_up"][0])),
        "ffn2_wd": _tile_w(f(inputs["ffn2_w_down"][0])),
        "w_in": _tile_w(win_main),
        "w_fa": np.ascontiguousarray(win[:, C_FA:C_FA + 8].reshape(KC, 128, 8).transpose(1, 0, 2)),
        "w_pf": _tile_w(f(inputs["mix_w_proj_fox"][0])), "w_ph": _tile_w(f(inputs["mix_w_proj_hgrn"][0])),
        "w_o": _tile_w(f(inputs["mix_w_out"][0])), "w_pg": _tile_w(f(inputs["ple_w_gate"][0])),
        "w_pp": _tile_w(f(inputs["ple_w_proj"][0])),
        "gains": np.ascontiguousarray(gains.reshape(8, KC, 128).transpose(2, 0, 1)),
        "fbias": f(np.asarray(inputs["fox_f_bias"])[0].reshape(NH, 1)),
        "lbl": np.ascontiguousarray(f(inputs["hgrn_lb_logits"]).reshape(2, NH, 128).transpose(2, 0, 1)),
        "gn": f(np.asarray(inputs["hgrn_norm_g"])[0].reshape(DH, 1)),
    }
    x = np.asarray(inputs["x"], dtype=np.float32)
    p = np.asarray(inputs["p"], dtype=np.float32)
    maps = []
    for c in range(ncores):
        m = dict(common)
        m["x"] = np.ascontiguousarray(x[c].T)
        m["p"] = np.ascontiguousarray(p[0, c].T)
        maps.append(m)
    return maps


def kernel(**inputs):
    nc = _get_nc()
    maps = make_in_maps(inputs, 8)
    res = run_bass_kernel_spmd(nc, maps, core_ids=list(range(8)))
    out = np.stack([np.ascontiguousarray(np.asarray(r["out"], dtype=np.float32).T) for r in res.results], axis=0)
    return out
```

```python
import numpy as np
from contextlib import ExitStack
import concourse.bass as bass
import concourse.mybir as mybir
from concourse.bass_utils import run_bass_kernel_spmd

F32 = mybir.dt.float32
BF16 = mybir.dt.bfloat16
AF = mybir.ActivationFunctionType
ALU = mybir.AluOpType

D = 2048
S = 2048
F = 5632
NH = 8
DH = 128
PLE = 256
INW = 11272
TB = 512
NB = S // TB
KC = D // 128
EPS = 1e-6
CELL = 1024
C_QA, C_KA, C_VA, C_FA, C_QB, C_FB, C_IB, C_GB, C_GA, C_GBT = 0, 1024, 2048, 3072, 3080, 4104, 5128, 6152, 7176, 9224
FGROUPS = [12, 10, 12, 10]
M_QA, M_KA, M_VA, M_QB, M_FB, M_IB, M_GB, M_GA, M_GBT = 0, 8, 16, 24, 32, 40, 48, 56, 72
NSLOT = 6


class O:
    __slots__ = ("ap", "cells")

    def __init__(self, ap, cells):
        self.ap = ap
        self.cells = cells


class Buf:
    def __init__(self, ap, fshape, esize, key, byte0=0, cell=CELL):
        self.ap = ap
        self.cell = cell
        self.fshape = tuple(fshape)
        self.es = esize
        self.key = key
        self.byte0 = byte0
        st = []
        acc = 1
        for n in reversed(self.fshape):
            st.append(acc)
            acc *= n
        self.strides = list(reversed(st))

    @property
    def cells(self):
        return self.s().cells

    def s(self, *idx, p=None):
        full = [slice(None)] * len(self.fshape)
        for i, x in enumerate(idx):
            full[i] = x
        lo = hi = 0
        for n, stp, x in zip(self.fshape, self.strides, full):
            if isinstance(x, int):
                lo += x * stp
                hi += x * stp
            else:
                a = x.start or 0
                b = n if x.stop is None else x.stop
                lo += a * stp
                hi += (b - 1) * stp
        c0 = (self.byte0 + lo * self.es) // self.cell
        c1 = (self.byte0 + hi * self.es + self.es - 1) // self.cell
        psl = slice(None) if p is None else slice(p[0], p[1])
        return O(self.ap[(psl,) + tuple(full)], [(self.key, c) for c in range(c0, c1 + 1)])


class Sched:
    ENG = ("pe", "act", "dve", "pool", "sp")

    def __init__(self, nc, es):
        self.nc = nc
        self.es = es
        self.q = {e: [] for e in self.ENG}
        self.sems = {}
        self.cnt = {}
        self.waited = {e: {} for e in self.ENG}
        self.lastw = {}
        self.readers = {}
        for e in ("pe", "act", "dve", "pool"):
            self.newsem(e)

    def newsem(self, key):
        self.sems[key] = self.es.enter_context(self.nc.semaphore("s_" + str(key)))
        self.cnt[key] = 0

    def _deps(self, eng, R, W):
        deps = {}

        def add(tok):
            k, v = tok
            if deps.get(k, 0) < v:
                deps[k] = v

        for c in R:
            t = self.lastw.get(c)
            if t is not None:
                add(t)
        for c in W:
            t = self.lastw.get(c)
            if t is not None:
                add(t)
            rd = self.readers.get(c)
            if rd:
                for k, v in rd.items():
                    add((k, v))
        waits = []
        wd = self.waited[eng]
        for k, v in deps.items():
            if k == "pe" and eng == "pe":
                continue
            if wd.get(k, 0) < v:
                wd[k] = v
                waits.append((k, v))
        return waits

    def _record(self, tok, R, W):
        for c in W:
            self.lastw[c] = tok
            self.readers[c] = {}
        k, v = tok
        for c in R:
            rd = self.readers.setdefault(c, {})
            if rd.get(k, 0) < v:
                rd[k] = v

    def op(self, eng, fn, R, W, inc=True):
        waits = self._deps(eng, R, W)
        if inc:
            self.cnt[eng] += 1
            tok = (eng, self.cnt[eng])
        else:
            tok = (eng, self.cnt[eng] + 1)
        self._record(tok, R, W)
        self.q[eng].append((waits, fn, (eng, 1) if inc else None))
        return tok

    def dma(self, queue, semkey, pairs, **kw):
        if semkey not in self.sems:
            self.newsem(semkey)
        R, W = [], []
        for o, i in pairs:
            if isinstance(o, O):
                W += o.cells
            if isinstance(i, O):
                R += i.cells
        waits = self._deps(queue, R, W)
        self.cnt[semkey] += 16 * len(pairs)
        tok = (semkey, self.cnt[semkey])
        self._record(tok, R, W)
        first = True
        for o, i in pairs:
            oa = o.ap if isinstance(o, O) else o
            ia = i.ap if isinstance(i, O) else i
            self.q[queue].append((waits if first else [], (lambda e, oa=oa, ia=ia: e.dma_start(out=oa, in_=ia, **kw)), (semkey, 16)))
            first = False
        return tok

    def final_wait(self, eng, keys):
        waits = [(k, self.cnt[k]) for k in keys if self.cnt[k] > 0]
        self.q[eng].append((waits, None, None))

    def emit(self, block):
        sems = self.sems

        def run(e, lst):
            for waits, fn, inc in lst:
                for k, v in waits:
                    e.wait_ge(sems[k], v)
                if fn is not None:
                    ins = fn(e)
                    if inc is not None:
                        ins.then_inc(sems[inc[0]], inc[1])

        q = self.q
        block.tensor(lambda e: run(e, q["pe"]))
        block.scalar(lambda e: run(e, q["act"]))
        block.vector(lambda e: run(e, q["dve"]))
        block.gpsimd(lambda e: run(e, q["pool"]))
        block.sync(lambda e: run(e, q["sp"]))


def build_program(debug=False, nblocks=NB, phases=("ffn1", "mix", "ffn2", "ple")):
    nc = bass.Bass("TRN2", target_bir_lowering=False)

    def din(name, shape):
        return nc.dram_tensor(name, list(shape), F32, kind="ExternalInput").ap()

    x_d = din("x", (D, S))
    p_d = din("p", (PLE, S))
    w = {}
    for pre in ("ffn1", "ffn2"):
        w[pre + "_wg"] = din(pre + "_wg", (F // 128, 128, KC, 128))
        w[pre + "_wu"] = din(pre + "_wu", (F // 128, 128, KC, 128))
        w[pre + "_wd"] = din(pre + "_wd", (D // 128, 128, F // 128, 128))
    w_in = din("w_in", ((INW - 8) // 128, 128, KC, 128))
    w_fa = din("w_fa", (128, KC, 8))
    w_pf = din("w_pf", (D // 128, 128, NH, 128))
    w_ph = din("w_ph", (D // 128, 128, NH, 128))
    w_o = din("w_o", (D // 128, 128, KC, 128))
    w_pg = din("w_pg", (D // 128, 128, KC, 128))
    w_pp = din("w_pp", (D // 128, 128, 2, 128))
    gain_names = ["ffn1_pre", "ffn1_post", "mix_pre", "mix_post", "ffn2_pre", "ffn2_post", "ple_pre", "ple_post"]
    gains_d = din("gains", (128, 8, KC))
    fbias_d = din("fbias", (NH, 1))
    lbl_d = din("lbl", (128, 2, NH))
    gn_d = din("gn", (DH, 1))
    out_d = nc.dram_tensor("out", [D, S], F32, kind="ExternalOutput").ap()
    dbg_d = None
    if debug:
        dbg_d = nc.dram_tensor("dbg", [8, 128, KC * TB], F32, kind="ExternalOutput").ap()

    with ExitStack() as es:
        sc = Sched(nc, es)
        nb_ctr = [0]

        def sbt(shape, dt, name=None):
            nb_ctr[0] += 1
            return es.enter_context(nc.sbuf_tensor(name or ("t%d" % nb_ctr[0]), list(shape), dt))

        def mkbuf(fshape, dt, name, parts=128, cell=CELL):
            t = sbt([parts] + list(fshape), dt, name)
            return Buf(t[:], fshape, 4 if dt == F32 else 2, name, 0, cell)

        hT = mkbuf((KC, TB), F32, "hT")
        KT = mkbuf((NH, S), BF16, "KT")
        VV = mkbuf((S // 128, NH * DH), BF16, "VV")
        wslots = [mkbuf((KC, 128), BF16, "wslot%d" % i) for i in range(NSLOT)]
        ARENA = 68 * 1024
        arena_t = sbt([128, ARENA // 4], F32, "arena")

        def av(off, fshape, dt, parts=128):
            es_ = 4 if dt == F32 else 2
            n = int(np.prod(fshape))
            nb = n * es_
            assert off % 4 == 0 and nb % 4 == 0 and off + nb <= ARENA, (off, nb)
            ap = arena_t[0:parts, off // 4:(off + nb) // 4]
            if dt != F32:
                ap = ap.bitcast(dt)
            if len(fshape) == 2:
                ap = ap.rearrange("p (a b) -> p a b", a=fshape[0])
            elif len(fshape) == 3:
                ap = ap.rearrange("p (a b c) -> p a b c", a=fshape[0], b=fshape[1])
            return Buf(ap, fshape, es_, "arena", off)

        K = 1024
        uT = av(0, (KC, TB), BF16)
        aT = av(16 * K, (12, TB), BF16)
        sgb = [av(28 * K + i * K, (TB,), BF16) for i in range(2)]
        sqb = [av(30 * K + i * K, (TB,), BF16) for i in range(2)]
        yb = av(32 * K, (KC, TB), F32)
        rstd = av(64 * K, (TB,), F32)
        tmpf = av(66 * K, (TB,), F32)
        sq_alt = [av(i * K, (TB,), BF16) for i in range(2)]
        QT = av(16 * K, (NH, TB), BF16)
        PTb = [av(24 * K + i * K, (TB,), BF16) for i in range(4)]
        cT = av(36 * K, (TB,), F32)
        spb = av(38 * K, (TB,), F32)
        recb = av(40 * K, (TB,), F32)
        chb = [av(42 * K + i * K, (TB,), BF16) for i in range(2)]
        yfox = av(56 * K, (NH, TB), BF16)
        QtT = av(16 * K, (4, TB), BF16)
        KtT = av(20 * K, (4, TB), BF16)
        Ktok = av(24 * K, (4, 4 * DH), BF16)
        Vh = av(28 * K, (4, 4 * DH), BF16)
        gsb = av(32 * K, (4, TB), BF16)
        tq = [av(36 * K + i * 2 * K, (TB,), F32) for i in range(6)]
        ohg = av(48 * K, (NH, TB), BF16)
        merged = av(16 * K, (KC, TB), BF16)
        sga = [av(32 * K + i * 2 * K, (TB,), F32) for i in range(2)]
        sgbb = [av(36 * K + i * 2 * K, (TB,), F32) for i in range(2)]
        m1 = av(40 * K, (TB,), F32)
        m2 = av(42 * K, (TB,), F32)
        sgp = [av(24 * K + i * 2 * K, (TB,), F32) for i in range(2)]

        ident = mkbuf((128,), F32, "ident")
        identb = mkbuf((128,), BF16, "identb")
        onesb = mkbuf((128,), BF16, "onesb")
        tri = mkbuf((128,), BF16, "tri")
        trif = mkbuf((128,), F32, "trif")
        hmask = mkbuf((128,), F32, "hmask")
        d0m = mkbuf((TB,), F32, "d0m")
        ones8 = mkbuf((128,), F32, "ones8", parts=8)
        onesf = mkbuf((TB,), F32, "onesf", parts=8)
        negI8 = mkbuf((8,), F32, "negI8", parts=8)
        oh8 = mkbuf((8,), F32, "oh8", parts=8)
        gains = mkbuf((8, KC), F32, "gains_sb")
        gsc = mkbuf((8, KC), F32, "gains_sc")
        lb = mkbuf((NH,), F32, "lb")
        oml = mkbuf((NH,), F32, "oml")
        ltmp = mkbuf((2, NH), F32, "ltmp")
        negfb = mkbuf((1,), F32, "negfb", parts=8)
        fbt = mkbuf((1,), F32, "fbt", parts=8)
        gn = mkbuf((1,), F32, "gn_sb")
        Sst = mkbuf((NH, DH), F32, "Sst", cell=512)
        Sbf = mkbuf((NH, DH), BF16, "Sbf", cell=256)
        negc = mkbuf((S // 128, NH), F32, "negc")
        carry = mkbuf((1,), F32, "carry", parts=8)
        PTm = mkbuf((4, 128), BF16, "PTm", cell=256)
        pT = mkbuf((2, TB), BF16, "pT")
        stmp = mkbuf((4, 128), F32, "stmp", cell=512)

        PS = []
        for i in range(8):
            t = es.enter_context(nc.psum_tensor("ps%d" % i, [128, TB], F32))
            PS.append(Buf(t[:], (TB,), 4, "ps%d" % i, 0, 2048))

        def ap_of(x):
            return x.ap if isinstance(x, O) else x

        def cells_of(*xs):
            r = []
            for x in xs:
                if isinstance(x, O):
                    r += x.cells
            return r

        def mm(out, lhsT, rhs, start=True, stop=True, inc=True, skip=False):
            R = lhsT.cells + rhs.cells
            sc.op("pe", lambda e: e.matmul(out.ap, lhsT.ap, rhs.ap, start=start, stop=stop, skip_group_check=skip),
                  R, out.cells, inc=inc)

        def act(out, in_, func, bias=None, scale=None, accum=None, eng="act"):
            kw = {}
            if bias is not None:
                kw["bias"] = ap_of(bias)
            if scale is not None:
                kw["scale"] = ap_of(scale)
            if accum is not None:
                kw["accum_out"] = accum.ap
            sc.op("act", lambda e: e.activation(out=out.ap, in_=in_.ap, func=func, **kw),
                  cells_of(in_, bias, scale), cells_of(out, accum))

        def tt(eng, out, a, b, op):
            sc.op(eng, lambda e: e.tensor_tensor(out=out.ap, in0=a.ap, in1=b.ap, op=op), cells_of(a, b), out.cells)

        def ts(eng, out, a, s1, s2, op0, op1=None):
            if op1 is None:
                sc.op(eng, lambda e: e.tensor_scalar(out=out.ap, in0=a.ap, scalar1=ap_of(s1), scalar2=None, op0=op0),
                      cells_of(a, s1), out.cells)
            else:
                sc.op(eng, lambda e: e.tensor_scalar(out=out.ap, in0=a.ap, scalar1=ap_of(s1), scalar2=ap_of(s2), op0=op0, op1=op1),
                      cells_of(a, s1, s2), out.cells)

        def stt(eng, out, a, scalar, b, op0, op1):
            sc.op(eng, lambda e: e.scalar_tensor_tensor(out=out.ap, in0=a.ap, scalar=ap_of(scalar), in1=b.ap, op0=op0, op1=op1),
                  cells_of(a, scalar, b), out.cells)

        def scan(eng, out, d0, d1, init, op0, op1):
            sc.op(eng, lambda e: e.tensor_tensor_scan(out.ap, d0.ap, d1.ap, ap_of(init), op0, op1),
                  cells_of(d0, d1, init), out.cells)

        def cp(eng, out, in_):
            if eng == "act":
                sc.op("act", lambda e: e.copy(out=out.ap, in_=in_.ap), in_.cells, out.cells)
            else:
                sc.op(eng, lambda e: e.tensor_copy(out=out.ap, in_=in_.ap), in_.cells, out.cells)

        def act_lnexp_tables():
            pass

        def memset(eng, out, val):
            sc.op(eng, lambda e: e.memset(out.ap, val), [], out.cells)

        def recip(out, in_):
            sc.op("dve", lambda e: e.reciprocal(out=out.ap, in_=in_.ap), in_.cells, out.cells)

        def transpose(out, in_):
            sc.op("pe", lambda e: e.transpose(out.ap, in_.ap, ident.ap[:, :]), in_.cells + ident.s().cells, out.cells)

        wctr = [0]

        prefetched = {}

        def wprefetch(specs):
            for sp_ in specs:
                key = (id(sp_[0]),) + tuple(sp_[1:])
                prefetched.setdefault(key, []).append(wload(*sp_, _nopf=True))

        def wload(tiled, t, k0=0, nk=KC, cols=128, _nopf=False):
            if not _nopf:
                key = (id(tiled), t, k0, nk, cols)
                if prefetched.get(key):
                    return prefetched[key].pop(0)
            i = wctr[0] % NSLOT
            wctr[0] += 1
            slot = wslots[i]
            src = tiled[t] if t is not None else tiled
            sc.dma("pool", "wsem%d" % i, [(slot.s(slice(0, nk), slice(0, cols)), src[:, k0:k0 + nk, :])])
            return slot

        memset("dve", ident.s(), 0.0)
        sc.op("pool", lambda e: e.affine_select(out=ident.ap[:, :], in_=ident.ap[:, :], pattern=[[-1, 128]],
                                                compare_op=ALU.not_equal, fill=1.0, base=0, channel_multiplier=1),
              ident.s().cells, ident.s().cells)
        cp("dve", identb.s(), ident.s())
        memset("dve", onesb.s(), 1.0)
        memset("dve", trif.s(), 0.0)
        sc.op("pool", lambda e: e.affine_select(out=trif.ap[:, :], in_=trif.ap[:, :], pattern=[[1, 128]],
                                                compare_op=ALU.is_ge, fill=-30000.0, base=0, channel_multiplier=-1),
              trif.s().cells, trif.s().cells)
        cp("dve", tri.s(), trif.s())
        memset("dve", hmask.s(), 1.0)
        sc.op("pool", lambda e: e.affine_select(out=hmask.ap[:, :], in_=hmask.ap[:, :], pattern=[[1, 128]],
                                                compare_op=ALU.is_ge, fill=0.0, base=0, channel_multiplier=-1),
              hmask.s().cells, hmask.s().cells)
        memset("dve", hmask.s(slice(64, 128), p=(0, 64)), 0.0)
        memset("dve", d0m.s(), 1.0)
        sc.op("dve", lambda e: e.memset(d0m.ap.rearrange("p (c t) -> p c t", t=64)[:, :, 0:1], 0.0), [], d0m.s().cells)
        memset("dve", ones8.s(), 1.0)
        memset("dve", onesf.s(), 1.0)
        memset("dve", negI8.s(), 0.0)
        sc.op("pool", lambda e: e.affine_select(out=negI8.ap[:, :], in_=negI8.ap[:, :], pattern=[[-1, 8]],
                                                compare_op=ALU.not_equal, fill=-1.0, base=0, channel_multiplier=1),
              negI8.s().cells, negI8.s().cells)
        ts("dve", oh8.s(), negI8.s(), -1.0, None, ALU.mult)
        memset("dve", Sst.s(), 0.0)
        memset("dve", Sbf.s(), 0.0)
        memset("dve", carry.s(), 0.0)
        sc.dma("sp", "small", [
            (gains.s(), gains_d),
            (ltmp.s(), lbl_d),
            (O(fbt.ap[0:8, :], fbt.s().cells), fbias_d),
            (gn.s(), gn_d),
        ])
        for gi, gname in enumerate(gain_names):
            scl = 0.5 if gname in ("ffn1_post", "ffn2_post") else 1.0
            ts("dve", gsc.s(gi), gains.s(gi), scl, None, ALU.mult)
        tt("dve", lb.s(), ltmp.s(1), ltmp.s(0), ALU.subtract)
        act(lb.s(), lb.s(), AF.Exp)
        ts("dve", oml.s(), lb.s(), 1.0, None, ALU.add)
        recip(lb.s(), oml.s())
        ts("dve", oml.s(), lb.s(), -1.0, 1.0, ALU.mult, ALU.add)
        ts("dve", negfb.s(), fbt.s(), -1.0, None, ALU.mult)

        GI = {n: i for i, n in enumerate(gain_names)}

        def rstd_from(ps_ss, n):
            act_lnexp_tables()
            act(rstd.s(), ps_ss, AF.Ln, bias=EPS, scale=1.0 / n)
            act(rstd.s(), rstd.s(), AF.Exp, scale=-0.5)

        def prenorm(gname):
            gi = GI[gname]
            pss = PS[7].s()
            for dc in range(KC):
                sq = sqb[dc % 2]
                act(sq.s(), hT.s(dc), AF.Square)
                mm(pss, onesb.s(), sq.s(), start=(dc == 0), stop=(dc == KC - 1))
            rstd_from(pss, D)
            for dc in range(KC):
                stt("dve", uT.s(dc), hT.s(dc), gains.s(gi, slice(dc, dc + 1)), rstd.s(), ALU.mult, ALU.mult)

        def ss_sq(dc, sqs=None):
            sq = (sqs or sqb)[dc % 2]
            act(sq.s(), yb.s(dc), AF.Square)

        def ss_mm(dc, sqs=None):
            sq = (sqs or sqb)[dc % 2]
            mm(PS[7].s(), onesb.s(), sq.s(), start=(dc == 0), stop=(dc == KC - 1))

        def ss_accum(dc, sqs=None, defer=2):
            if dc - defer >= 0:
                ss_mm(dc - defer, sqs)
            ss_sq(dc, sqs)
            if dc == KC - 1:
                for d2 in range(max(0, KC - defer), KC):
                    ss_mm(d2, sqs)

        def postnorm(gname, next_specs=()):
            gi = GI[gname]
            wprefetch(next_specs)
            rstd_from(PS[7].s(), D)
            for d2 in range(0, KC, 2):
                for dc in (d2, d2 + 1):
                    stt("dve", yb.s(dc), yb.s(dc), gsc.s(gi, slice(dc, dc + 1)), rstd.s(), ALU.mult, ALU.mult)
                tt("dve", hT.s(slice(d2, d2 + 2)), hT.s(slice(d2, d2 + 2)), yb.s(slice(d2, d2 + 2)), ALU.add)

        def proj_fm(ps, wslot, col0, ncols, rhs_buf, nk):
            for kc in range(nk):
                mm(O(ps.ap[0:ncols, :], ps.cells), wslot.s(kc, slice(col0, col0 + ncols)), rhs_buf.s(kc),
                   start=(kc == 0), stop=(kc == nk - 1), inc=(kc == nk - 1))

        def ffn(pre, b, next_specs=()):
            wg, wu, wd = w[pre + "_wg"], w[pre + "_wu"], w[pre + "_wd"]
            prenorm(pre + "_pre")
            f0 = 0
            pi = 0
            for g, gsz in enumerate(FGROUPS):
                fl_start = 0
                if g == 0:
                    sl4 = [wload(wg, 0), wload(wu, 0), wload(wg, 1), wload(wu, 1)]
                    for kc in range(KC):
                        for i4 in range(4):
                            mm(PS[i4].s(), sl4[i4].s(kc), uT.s(kc), start=(kc == 0), stop=(kc == KC - 1), inc=(kc == KC - 1))
                    for fl in range(2):
                        sgt = sgb[fl % 2]
                        act(sgt.s(), PS[2 * fl].s(), AF.Silu)
                        tt("dve", aT.s(fl), PS[2 * fl + 1].s(), sgt.s(), ALU.mult)
                    fl_start = 2
                for fl in range(fl_start, gsz):
                    sg_ = wload(wg, f0 + fl)
                    su_ = wload(wu, f0 + fl)
                    pg = PS[(pi % 2) * 2]
                    pu = PS[(pi % 2) * 2 + 1]
                    pi += 1
                    proj_fm(pg.s(), sg_, 0, 128, uT, KC)
                    proj_fm(pu.s(), su_, 0, 128, uT, KC)
                    sgt = sgb[fl % 2]
                    act(sgt.s(), pg.s(), AF.Silu)
                    tt("dve", aT.s(fl), pu.s(), sgt.s(), ALU.mult)
                for dc in range(KC):
                    sd_ = wload(wd, dc, f0, gsz)
                    py = PS[4 + (dc % 2)]
                    for fl in range(gsz):
                        mm(py.s(), sd_.s(fl), aT.s(fl), start=(fl == 0), stop=(fl == gsz - 1), inc=(fl == gsz - 1))
                    if g == 0:
                        cp("act", yb.s(dc), py.s())
                    else:
                        tt("dve", yb.s(dc), yb.s(dc), py.s(), ALU.add)
                    if g == len(FGROUPS) - 1:
                        ss_accum(dc)
                f0 += gsz
            postnorm(pre + "_post", next_specs)

        def load_x(b):
            t0 = b * TB
            for g4 in range(4):
                src = x_d[g4 * 512:(g4 + 1) * 512, t0:t0 + TB].rearrange("(c p) t -> p c t", p=128)
                sc.dma("sp", "xs%d" % g4, [(hT.s(slice(g4 * 4, g4 * 4 + 4)), src)])

        def store_out(b):
            t0 = b * TB
            for g4 in range(4):
                dst = out_d[g4 * 512:(g4 + 1) * 512, t0:t0 + TB].rearrange("(c p) t -> p c t", p=128)
                sc.dma("sp", "os%d" % g4, [(dst, hT.s(slice(g4 * 4, g4 * 4 + 4)))])

        def dump(slot):
            if debug:
                sc.dma("sp", "dbg", [(dbg_d[slot], O(hT.ap.rearrange("p a b -> p (a b)"), hT.s().cells))])

        def mixer(b, next_specs=()):
            t0 = b * TB
            prenorm("mix_pre")
            sfa = wload(w_fa, None, 0, KC, 8)
            sk3 = [wload(w_in, M_KA + h) for h in range(3)]
            pf = PS[6]
            for kc in range(KC):
                mm(O(pf.ap[0:8, :], pf.cells), sfa.s(kc, slice(0, 8)), uT.s(kc), start=(kc == 0), stop=(kc == KC - 1), inc=(kc == KC - 1))
                for h in range(3):
                    mm(PS[h].s(), sk3[h].s(kc), uT.s(kc), start=(kc == 0), stop=(kc == KC - 1), inc=(kc == KC - 1))
            for h in range(3):
                cp("act", KT.s(h, slice(t0, t0 + TB)), PS[h].s())
            sp8 = O(spb.ap[0:8, :], spb.s().cells)
            cT8 = O(cT.ap[0:8, :], cT.s().cells)
            act_lnexp_tables()
            act(sp8, O(pf.ap[0:8, :], pf.cells), AF.Exp, bias=O(negfb.ap[0:8, :], negfb.s().cells), scale=-1.0)
            act(sp8, sp8, AF.Ln, bias=1.0, scale=1.0)
            scan("dve", cT8, O(onesf.ap[0:8, :], onesf.s().cells), sp8, O(carry.ap[0:8, :], carry.s().cells), ALU.mult, ALU.subtract)
            cp("dve", O(carry.ap[0:8, :], carry.s().cells), O(cT.ap[0:8, TB - 1:TB], cT.s().cells))
            for j in range(4):
                pn = PS[4 + (j % 2)]
                mm(O(pn.ap[:, 0:8], pn.cells), O(cT.ap[0:8, j * 128:(j + 1) * 128], cT.s().cells),
                   O(negI8.ap[0:8, :], negI8.s().cells))
                cp("dve", negc.s(b * 4 + j), O(pn.ap[:, 0:8], pn.cells))
            pi = 3
            for h in range(3, NH):
                sk_ = wload(w_in, M_KA + h)
                ps = PS[pi % 4]
                pi += 1
                proj_fm(ps.s(), sk_, 0, 128, uT, KC)
                cp("act", KT.s(h, slice(t0, t0 + TB)), ps.s())
            for h in range(NH):
                sq_ = wload(w_in, M_QA + h)
                ps = PS[pi % 4]
                pi += 1
                proj_fm(ps.s(), sq_, 0, 128, uT, KC)
                act(QT.s(h), ps.s(), AF.Copy, scale=float(DH ** -0.5))
            for h in range(NH):
                sv_ = wload(w_in, M_VA + h)
                ps = PS[pi % 4]
                pi += 1
                for j in range(4):
                    po = ps.s(slice(j * 128, (j + 1) * 128))
                    for kc in range(KC):
                        mm(po, uT.s(kc, slice(j * 128, (j + 1) * 128)), sv_.s(kc), start=(kc == 0), stop=(kc == KC - 1), inc=(kc == KC - 1))
                cp("dve" if h % 2 else "act", VV.s(slice(b * 4, b * 4 + 4), slice(h * 128, (h + 1) * 128)),
                   O(ps.ap.rearrange("p (a t) -> p a t", a=4), ps.cells))
            nkt = 4 * (b + 1)
            LA = 3
            SBK = [PS[0], PS[1], PS[4], PS[5]]

            def make_ch(h):
                ch = chb[h % 2]
                ts("dve", O(ch.ap[0:8, :], ch.cells), cT8, O(oh8.ap[0:8, h:h + 1], oh8.cells), None, ALU.mult)

            make_ch(0)
            for h in range(NH):
                chh = chb[h % 2]
                po = PS[2] if h % 2 == 0 else PS[6]
                pd = PS[3] if h % 2 == 0 else PS[7]

                def scores(j):
                    r = j - 4 * b
                    col0 = 128 * r if r > 0 else 0
                    cs = slice(col0, TB)
                    ps = SBK[j % 4]
                    kt = KT.s(h, slice(j * 128, (j + 1) * 128))
                    diag = r >= 0
                    mm(ps.s(cs), kt, QT.s(h, cs), start=True, stop=False, inc=False, skip=True)
                    mm(ps.s(cs), O(onesb.ap[0:8, :], onesb.cells), O(chh.ap[0:8, cs], chh.cells),
                       start=False, stop=(not diag), inc=(not diag), skip=True)
                    if diag:
                        mm(ps.s(slice(col0, col0 + 128)), identb.s(), tri.s(), start=False, stop=True, inc=True, skip=True)
                    return col0

                col0s = {}
                for j in range(min(LA, nkt)):
                    col0s[j] = scores(j)
                if h + 1 < NH:
                    make_ch(h + 1)
                for j in range(nkt):
                    if j + LA < nkt:
                        col0s[j + LA] = scores(j + LA)
                    col0 = col0s[j]
                    cs = slice(col0, TB)
                    ps = SBK[j % 4]
                    PT_ = PTb[j % 4]
                    act(PT_.s(cs), ps.s(cs), AF.Exp, bias=negc.s(j, slice(h, h + 1)))
                    last = (j == nkt - 1)
                    mm(po.s(cs), VV.s(j, slice(h * 128, (h + 1) * 128)), PT_.s(cs), start=(j == 0), stop=last, inc=True, skip=True)
                    mm(pd.s(cs), onesb.s(), PT_.s(cs), start=(j == 0), stop=last, inc=True, skip=True)
                recip(recb.s(), pd.s())
                tt("dve", yfox.s(h), po.s(), recb.s(), ALU.mult)

            for hh in range(2):
                pi = 0
                sets = [tq[0:4], [tq[4], tq[5], rstd, tmpf]]
                for pr2 in range(2):
                    hls = (2 * pr2, 2 * pr2 + 1)
                    for i2, hl in enumerate(hls):
                        h = hh * 4 + hl
                        sq_ = wload(w_in, M_QB + h)
                        sf_ = wload(w_in, M_FB + h)
                        proj_fm(PS[i2].s(), sq_, 0, 128, uT, KC)
                        proj_fm(PS[2 + i2].s(), sf_, 0, 128, uT, KC)
                    for i2, hl in enumerate(hls):
                        tA, tB, tC, tD = sets[i2]
                        act(tA.s(), PS[2 + i2].s(), AF.Sigmoid)
                        act(tD.s(), PS[i2].s(), AF.Sigmoid)
                    for i2, hl in enumerate(hls):
                        h = hh * 4 + hl
                        tA, tB, tC, tD = sets[i2]
                        ts("dve", tA.s(), tA.s(), oml.s(slice(h, h + 1)), lb.s(slice(h, h + 1)), ALU.mult, ALU.add)
                        tt("dve", tD.s(), PS[i2].s(), tD.s(), ALU.mult)
                    act_lnexp_tables()
                    for i2, hl in enumerate(hls):
                        tA, tB, tC, tD = sets[i2]
                        act(tB.s(), tA.s(), AF.Ln)
                        scan("dve", tC.s(), d0m.s(), tB.s(), 0.0, ALU.mult, ALU.add)
                        ts("dve", tA.s(), tA.s(), -1.0, 1.0, ALU.mult, ALU.add)
                        act(tB.s(), tC.s(), AF.Exp)
                        sc.op("dve", lambda e, hl=hl, tB=tB: e.tensor_copy(
                            out=stmp.ap[:, hl, 0:8], in_=tB.ap.rearrange("p (c t) -> p c t", t=64)[:, :, 63]),
                            tB.s().cells, stmp.s(hl).cells)
                        tt("dve", QtT.s(hl), tD.s(), tB.s(), ALU.mult)
                        act(tB.s(), tC.s(), AF.Exp, scale=-1.0)
                        tt("dve", KtT.s(hl), tA.s(), tB.s(), ALU.mult)
                for hl in range(4):
                    si_ = wload(w_in, M_IB + hh * 4 + hl)
                    ps = PS[pi % 4]
                    pi += 1
                    for j in range(4):
                        po = ps.s(slice(j * 128, (j + 1) * 128))
                        for kc in range(KC):
                            mm(po, uT.s(kc, slice(j * 128, (j + 1) * 128)), si_.s(kc), start=(kc == 0), stop=(kc == KC - 1), inc=(kc == KC - 1))
                    cp("dve" if hl % 2 else "act", Vh.s(slice(0, 4), slice(hl * 128, (hl + 1) * 128)),
                       O(ps.ap.rearrange("p (a t) -> p a t", a=4), ps.cells))
                for hl in range(4):
                    ps = PS[pi % 4]
                    pi += 1
                    for j in range(4):
                        mm(ps.s(slice(j * 128, (j + 1) * 128)), KtT.s(hl, slice(j * 128, (j + 1) * 128)), identb.s())
                    cp("act" if hl % 2 else "dve", Ktok.s(slice(0, 4), slice(hl * 128, (hl + 1) * 128)),
                       O(ps.ap.rearrange("p (a t) -> p a t", a=4), ps.cells))
                S4 = Sst.s(slice(hh * 4, hh * 4 + 4))
                Sb4 = Sbf.s(slice(hh * 4, hh * 4 + 4))
                hm_b = O(hmask.ap.unsqueeze(1).to_broadcast([128, 4, 128]), hmask.cells)
                gslot = [None]
                for j in range(4):
                    bs = PS[6]
                    for hl in range(4):
                        mm(bs.s(slice(hl * 128, (hl + 1) * 128)), KtT.s(hl, slice(j * 128, (j + 1) * 128)),
                           QtT.s(hl, slice(j * 128, (j + 1) * 128)))
                    tt("dve", PTm.s(), O(bs.ap.rearrange("p (a t) -> p a t", a=4), bs.cells), hm_b, ALU.mult)
                    for c in range(2):
                        pr = (0, 64) if c == 0 else (64, 128)
                        bd = PS[c]
                        for hl in range(4):
                            mm(bd.s(slice(hl * 128, (hl + 1) * 128)), Ktok.s(j, slice(hl * 128, (hl + 1) * 128), p=pr),
                               Vh.s(j, slice(hl * 128, (hl + 1) * 128), p=pr))
                        st_ = j * 2 + c
                        hg, part = st_ // 2, st_ % 2
                        if part == 0:
                            gslot[0] = wload(w_in, M_GB + hh * 4 + hg)
                        for kc in range(part * 8, part * 8 + 8):
                            mm(PS[7].s(), gslot[0].s(kc), uT.s(kc), start=(kc == 0), stop=(kc == KC - 1), inc=(kc == KC - 1))
                        if part == 1:
                            act(gsb.s(hg), PS[7].s(), AF.Silu)
                        cols = slice(j * 128 + c * 64, j * 128 + c * 64 + 64)
                        pcols = slice(c * 64, c * 64 + 64)
                        for hl in range(4):
                            h = hh * 4 + hl
                            pov = PS[2 + hl].s(cols)
                            mm(pov, Vh.s(j, slice(hl * 128, (hl + 1) * 128), p=pr), PTm.s(hl, pcols, p=pr), start=True, stop=False, inc=False)
                            mm(pov, Sbf.s(h), QtT.s(hl, cols), start=False, stop=True, inc=True)
                        ci = j * 2 + c
                        el_b = O(stmp.ap[:, :, ci:ci + 1].to_broadcast([128, 4, 128]), stmp.cells)
                        tt("dve", S4, S4, O(bd.ap.rearrange("p (a t) -> p a t", a=4), bd.cells), ALU.add)
                        tt("dve", Sb4, S4, el_b, ALU.mult)
                        tt("dve", S4, S4, el_b, ALU.mult)
                act_lnexp_tables()
                for hl in range(4):
                    h = hh * 4 + hl
                    pov = PS[2 + hl]
                    osb = tq[0] if hl % 2 == 0 else tq[2]
                    rs = tq[1] if hl % 2 == 0 else tq[3]
                    sq = sqb[hl % 2]
                    act(sq.s(), pov.s(), AF.Square)
                    pss = PS[hl % 2]
                    mm(pss.s(), onesb.s(), sq.s())
                    act(rs.s(), pss.s(), AF.Ln, bias=EPS, scale=1.0 / DH)
                    act(rs.s(), rs.s(), AF.Exp, scale=-0.5)
                    stt("dve", osb.s(), pov.s(), gn.s(), rs.s(), ALU.mult, ALU.mult)
                    tt("dve", ohg.s(h), osb.s(), gsb.s(hl), ALU.mult)

            for dc in range(KC):
                q = dc % 2
                sa_ = wload(w_in, M_GA + dc)
                spf = wload(w_pf, dc, 0, NH)
                p1, p2 = PS[2 * q], PS[2 * q + 1]
                proj_fm(p1.s(), sa_, 0, 128, uT, KC)
                proj_fm(p2.s(), spf, 0, 128, yfox, NH)
                ga = sga[q]
                act(ga.s(), p1.s(), AF.Sigmoid)
                mq = m1 if q == 0 else m2
                tt("dve", mq.s(), p2.s(), ga.s(), ALU.mult)
                sb_ = wload(w_in, M_GBT + dc)
                sph = wload(w_ph, dc, 0, NH)
                p3, p4 = PS[4 + 2 * q], PS[5 + 2 * q]
                proj_fm(p3.s(), sb_, 0, 128, uT, KC)
                proj_fm(p4.s(), sph, 0, 128, ohg, NH)
                gb_ = sgbb[q]
                act(gb_.s(), p3.s(), AF.Sigmoid)
                tt("dve", gb_.s(), p4.s(), gb_.s(), ALU.mult)
                tt("dve", merged.s(dc), mq.s(), gb_.s(), ALU.add)
            if "nowout" in phases:
                return
            for dc in range(KC):
                so_ = wload(w_o, dc)
                py = PS[dc % 4]
                proj_fm(py.s(), so_, 0, 128, merged, KC)
                cp("act", yb.s(dc), py.s())
                ss_accum(dc, sq_alt)
            postnorm("mix_post", next_specs)

        def ple(b, next_specs=()):
            t0 = b * TB
            prenorm("ple_pre")
            for dc in range(KC):
                sg_ = wload(w_pg, dc)
                sp_ = wload(w_pp, dc, 0, 2)
                p1 = PS[(dc % 2) * 2]
                p2 = PS[(dc % 2) * 2 + 1]
                proj_fm(p1.s(), sg_, 0, 128, uT, KC)
                proj_fm(p2.s(), sp_, 0, 128, pT, 2)
                sg1 = sgp[dc % 2]
                act(sg1.s(), p1.s(), AF.Sigmoid)
                tt("dve", yb.s(dc), p2.s(), sg1.s(), ALU.mult)
                ss_accum(dc)
            postnorm("ple_post", next_specs)

        for b in range(nblocks):
            load_x(b)
            sc.dma("pool", "pld", [(pT.s(), p_d[:, b * TB:(b + 1) * TB].rearrange("(k p) t -> p k t", p=128))])
            if "ffn1" in phases:
                ffn("ffn1", b, [(w_fa, None, 0, KC, 8)] + [(w_in, M_KA + h_, 0, KC, 128) for h_ in range(3)])
            if b == 0:
                dump(0)
            if "mix" in phases:
                mixer(b, [(w["ffn2_wg"], 0, 0, KC, 128), (w["ffn2_wu"], 0, 0, KC, 128), (w["ffn2_wg"], 1, 0, KC, 128), (w["ffn2_wu"], 1, 0, KC, 128)])
            if b == 0:
                dump(1)
            if "ffn2" in phases:
                ffn("ffn2", b, [(w_pg, 0, 0, KC, 128), (w_pp, 0, 0, 2, 128), (w_pg, 1, 0, KC, 128), (w_pp, 1, 0, 2, 128)])
            if b == 0:
                dump(2)
            if "ple" in phases:
                ple(b, [(w["ffn1_wg"], 0, 0, KC, 128), (w["ffn1_wu"], 0, 0, KC, 128), (w["ffn1_wg"], 1, 0, KC, 128), (w["ffn1_wu"], 1, 0, KC, 128)] if b + 1 < nblocks else [])
            if b == 0:
                dump(3)
            store_out(b)
        fin = ["os0", "os1", "os2", "os3"] + (["dbg"] if debug else [])
        sc.final_wait("sp", fin)

        block = es.enter_context(nc.Block())
        sc.emit(block)
    return nc


_NC_CACHE = {}


def _get_nc(**kw):
    key = tuple(sorted((k, str(v)) for k, v in kw.items()))
    if key not in _NC_CACHE:
        _NC_CACHE[key] = build_program(**kw)
    return _NC_CACHE[key]


def _tile_w(W, cols=128):
    K_, N_ = W.shape
    return np.ascontiguousarray(W.reshape(K_ // 128, 128, N_ // cols, cols).transpose(2, 1, 0, 3))


def make_in_maps(inputs, ncores=8):
    f = lambda a: np.ascontiguousarray(np.asarray(a, dtype=np.float32))
    win = f(inputs["mix_w_in"][0])
    win_main = np.concatenate([win[:, :C_FA], win[:, C_FA + 8:]], axis=1)
    gnames = ["ffn1_pre", "ffn1_post", "mix_pre", "mix_post", "ffn2_pre", "ffn2_post", "ple_pre", "ple_post"]
    gains = np.stack([np.asarray(inputs[n + "_g"], dtype=np.float32)[0] for n in gnames], axis=0)
    common = {
        "ffn1_wg": _tile_w(f(inputs["ffn1_w_gate"][0])), "ffn1_wu": _tile_w(f(inputs["ffn1_w_up"][0])),
        "ffn1_wd": _tile_w(f(inputs["ffn1_w_down"][0])),
        "ffn2_wg": _tile_w(f(inputs["ffn2_w_gate"][0])), "ffn2_wu": _tile_w(f(inputs["ffn2_w_up"][0])),
        "ffn2_wd": _tile_w(f(inputs["ffn2_w_down"][0])),
        "w_in": _tile_w(win_main),
        "w_fa": np.ascontiguousarray(win[:, C_FA:C_FA + 8].reshape(KC, 128, 8).transpose(1, 0, 2)),
        "w_pf": _tile_w(f(inputs["mix_w_proj_fox"][0])), "w_ph": _tile_w(f(inputs["mix_w_proj_hgrn"][0])),
        "w_o": _tile_w(f(inputs["mix_w_out"][0])), "w_pg": _tile_w(f(inputs["ple_w_gate"][0])),
        "w_pp": _tile_w(f(inputs["ple_w_proj"][0])),
        "gains": np.ascontiguousarray(gains.reshape(8, KC, 128).transpose(2, 0, 1)),
        "fbias": f(np.asarray(inputs["fox_f_bias"])[0].reshape(NH, 1)),
        "lbl": np.ascontiguousarray(f(inputs["hgrn_lb_logits"]).reshape(2, NH, 128).transpose(2, 0, 1)),
        "gn": f(np.asarray(inputs["hgrn_norm_g"])[0].reshape(DH, 1)),
    }
    x = np.asarray(inputs["x"], dtype=np.float32)
    p = np.asarray(inputs["p"], dtype=np.float32)
    maps = []
    for c in range(ncores):
        m = dict(common)
        m["x"] = np.ascontiguousarray(x[c].T)
        m["p"] = np.ascontiguousarray(p[0, c].T)
        maps.append(m)
    return maps


def kernel(**inputs):
    nc = _get_nc()
    maps = make_in_maps(inputs, 8)
    res = run_bass_kernel_spmd(nc, maps, core_ids=list(range(8)))
    out = np.stack([np.ascontiguousarray(np.asarray(r["out"], dtype=np.float32).T) for r in res.results], axis=0)
    return out
```
